# Optimizing a Trainium2 kernel written in Bass

```python
import jax, jax.numpy as jnp
from jax import lax
import numpy as np

D_MODEL = 2048
BATCH = 2
SEQ = 16384
DEPTH = 2

N_META = 16
CONV_WIDTH = D_MODEL
CONV_KERNEL = 31
RET_HEADS = 8
RET_QK_DIM = D_MODEL // RET_HEADS
RET_V_DIM = 2 * D_MODEL // RET_HEADS
RET_QK_WIDTH = RET_HEADS * RET_QK_DIM
RET_WIDTH = RET_HEADS * RET_V_DIM
CHUNK = 128
ROPE_BASE = 10000.0
EPS = 1e-6

OFF_GLU_A = 0
OFF_GLU_B = OFF_GLU_A + CONV_WIDTH
OFF_CONV_GATE = OFF_GLU_B + CONV_WIDTH
OFF_Q = OFF_CONV_GATE + CONV_WIDTH
OFF_K = OFF_Q + RET_QK_WIDTH
OFF_V = OFF_K + RET_QK_WIDTH
OFF_RET_GATE = OFF_V + RET_WIDTH
OFF_MERGE_A = OFF_RET_GATE + RET_WIDTH
OFF_MERGE_B = OFF_MERGE_A + D_MODEL
IN_COLS = OFF_MERGE_B + D_MODEL

kernel_name = "hybrid_conformer_retention_gated"


def rmsnorm(x, g):
    xf = x.astype(jnp.float32)
    y = xf * lax.rsqrt(jnp.mean(xf * xf, axis=-1, keepdims=True) + EPS)
    return (y * g.astype(jnp.float32)).astype(x.dtype)


def layernorm(x, w, b):
    xf = x.astype(jnp.float32)
    mu = jnp.mean(xf, axis=-1, keepdims=True)
    var = jnp.mean(jnp.square(xf - mu), axis=-1, keepdims=True)
    y = (xf - mu) * lax.rsqrt(var + EPS)
    return (y * w.astype(jnp.float32) + b.astype(jnp.float32)).astype(x.dtype)


def causal_depthwise_conv(u, w, b):
    out = lax.conv_general_dilated(
        u, w[:, None, :].astype(u.dtype), window_strides=(1,),
        padding=[(CONV_KERNEL - 1, 0)],
        dimension_numbers=("NWC", "WIO", "NWC"),
        feature_group_count=u.shape[-1])
    return out + b.astype(u.dtype)


def conformer_branch(a, gl, gate, conv_w, conv_b, ln_w, ln_b, w_proj):
    u = a * jax.nn.sigmoid(gl)
    u = causal_depthwise_conv(u, conv_w, conv_b)
    u = layernorm(u, ln_w, ln_b)
    u = jax.nn.silu(u) * jax.nn.silu(gate)
    return u @ w_proj


def rotate_pairs(x, cos, sin):
    xr = x.reshape(x.shape[:-1] + (x.shape[-1] // 2, 2))
    x0, x1 = xr[..., 0], xr[..., 1]
    out = jnp.stack([x0 * cos - x1 * sin, x1 * cos + x0 * sin], axis=-1)
    return out.reshape(x.shape)


def retention_chunkwise(q, k, v):
    b, L, h, dk = q.shape
    dv = v.shape[-1]
    pad = (-L) % CHUNK
    padw = ((0, 0), (pad, 0), (0, 0), (0, 0))
    q, k, v = jnp.pad(q, padw), jnp.pad(k, padw), jnp.pad(v, padw)
    n = (L + pad) // CHUNK

    def to_chunks(t):
        return t.reshape(b, n, CHUNK, h, t.shape[-1]).transpose(1, 0, 3, 2, 4)

    qs, ks, vs = to_chunks(q), to_chunks(k), to_chunks(v)

    log_g = jnp.log(1.0 - jnp.exp2(-5.0 - jnp.arange(h, dtype=jnp.float32)))
    idx = jnp.arange(CHUNK, dtype=jnp.float32)
    dist = idx[:, None] - idx[None, :]
    decay_in = jnp.where(dist[None] >= 0,
                         jnp.exp(log_g[:, None, None] * jnp.maximum(dist, 0.0)[None]),
                         0.0)
    xi = jnp.exp(log_g[:, None] * (idx + 1.0)[None])
    zeta = jnp.exp(log_g[:, None] * (CHUNK - 1.0 - idx)[None])
    g_chunk = jnp.exp(log_g * CHUNK)

    def step(state, xs):
        qc, kc, vc = xs
        s = jnp.einsum("bhid,bhjd->bhij", qc, kc) * decay_in[None]
        inner = jnp.einsum("bhij,bhjv->bhiv", s, vc)
        cross = jnp.einsum("bhid,bhdv->bhiv", qc, state) * xi[None, :, :, None]
        state = state * g_chunk[None, :, None, None] + jnp.einsum(
            "bhjd,bhjv->bhdv", kc * zeta[None, :, :, None], vc)
        return state, inner + cross

    state0 = jnp.zeros((b, h, dk, dv), jnp.float32)
    _, o = lax.scan(step, state0, (qs, ks, vs))
    o = o.transpose(1, 0, 3, 2, 4).reshape(b, n * CHUNK, h, dv)
    return o[:, pad:]


def head_groupnorm(o, w, b):
    mu = jnp.mean(o, axis=-1, keepdims=True)
    var = jnp.mean(jnp.square(o - mu), axis=-1, keepdims=True)
    y = ((o - mu) * lax.rsqrt(var + EPS)).reshape(o.shape[0], o.shape[1], -1)
    return y * w.astype(jnp.float32) + b.astype(jnp.float32)


def setup_inputs(seed: int = 0) -> dict:
    key = jax.random.key(seed)
    ks = jax.random.split(key, 16)
    f32 = jnp.float32
    nrm = lambda k, shape, scale: (jax.random.normal(k, shape, f32) * scale)
    return {
        "x": nrm(ks[0], (BATCH, SEQ, D_MODEL), 1.0),
        "meta_tokens": nrm(ks[1], (N_META, D_MODEL), 1.0),
        "norm_g": 1.0 + nrm(ks[2], (DEPTH, D_MODEL), 0.02),
        "w_in": nrm(ks[3], (DEPTH, D_MODEL, IN_COLS), D_MODEL ** -0.5),
        "conv_w": nrm(ks[4], (DEPTH, CONV_KERNEL, CONV_WIDTH), CONV_KERNEL ** -0.5),
        "conv_b": nrm(ks[5], (DEPTH, CONV_WIDTH), 0.02),
        "conv_ln_w": 1.0 + nrm(ks[6], (DEPTH, CONV_WIDTH), 0.02),
        "conv_ln_b": nrm(ks[7], (DEPTH, CONV_WIDTH), 0.02),
        "conv_out": nrm(ks[8], (DEPTH, CONV_WIDTH, D_MODEL), CONV_WIDTH ** -0.5),
        "ret_gn_w": 1.0 + nrm(ks[9], (DEPTH, RET_WIDTH), 0.02),
        "ret_gn_b": nrm(ks[10], (DEPTH, RET_WIDTH), 0.02),
        "ret_out": nrm(ks[11], (DEPTH, RET_WIDTH, D_MODEL), RET_WIDTH ** -0.5),
        "w_out": nrm(ks[12], (DEPTH, D_MODEL, D_MODEL), D_MODEL ** -0.5),
        "final_norm_g": 1.0 + nrm(ks[13], (D_MODEL,), 0.02),
    }


def reference(x, meta_tokens, norm_g, w_in, conv_w, conv_b, conv_ln_w, conv_ln_b,
              conv_out, ret_gn_w, ret_gn_b, ret_out, w_out, final_norm_g):
    b = x.shape[0]
    meta = jnp.broadcast_to(meta_tokens[None].astype(x.dtype), (b, N_META, D_MODEL))
    h_res = jnp.concatenate([meta, x], axis=1)
    L = h_res.shape[1]

    pos = jnp.arange(L, dtype=jnp.float32)
    inv_freq = 1.0 / (ROPE_BASE ** jnp.linspace(0.0, 1.0, RET_QK_DIM // 2, dtype=jnp.float32))
    ang = pos[:, None] * inv_freq[None]
    cos = jnp.cos(ang)[None, :, None, :]
    sin = jnp.sin(ang)[None, :, None, :]

    for l in range(DEPTH):
        hn = rmsnorm(h_res, norm_g[l])
        p = hn @ w_in[l]
        glu_a = p[..., OFF_GLU_A:OFF_GLU_B]
        glu_b = p[..., OFF_GLU_B:OFF_CONV_GATE]
        conv_gate = p[..., OFF_CONV_GATE:OFF_Q]
        q = p[..., OFF_Q:OFF_K].reshape(b, L, RET_HEADS, RET_QK_DIM)
        k = p[..., OFF_K:OFF_V].reshape(b, L, RET_HEADS, RET_QK_DIM)
        v = p[..., OFF_V:OFF_RET_GATE].reshape(b, L, RET_HEADS, RET_V_DIM)
        ret_gate = p[..., OFF_RET_GATE:OFF_MERGE_A]
        merge_a = p[..., OFF_MERGE_A:OFF_MERGE_B]
        merge_b = p[..., OFF_MERGE_B:IN_COLS]

        y_a = conformer_branch(glu_a, glu_b, conv_gate, conv_w[l], conv_b[l],
                               conv_ln_w[l], conv_ln_b[l], conv_out[l])

        qf = rotate_pairs(q.astype(jnp.float32), cos, sin) * (RET_QK_DIM ** -0.5)
        kf = rotate_pairs(k.astype(jnp.float32), cos, sin)
        o = retention_chunkwise(qf, kf, v.astype(jnp.float32))
        o = head_groupnorm(o, ret_gn_w[l], ret_gn_b[l]).astype(h_res.dtype)
        y_b = (o * jax.nn.silu(ret_gate)) @ ret_out[l]

        merged = jax.nn.sigmoid(merge_a) * y_a + jax.nn.sigmoid(merge_b) * y_b
        h_res = h_res + merged @ w_out[l]

    y = rmsnorm(h_res, final_norm_g)
    return y[:, N_META:]
```

```python
import numpy as np
from contextlib import ExitStack
import concourse.bass as bass
import concourse.mybir as mybir
from concourse.bass_utils import run_bass_kernel_spmd

F32 = mybir.dt.float32
BF16 = mybir.dt.bfloat16
AF = mybir.ActivationFunctionType
ALU = mybir.AluOpType
EPS = 1e-6
ENGINES = ("pe", "act", "dve", "pool", "sp")


class Emit:
    def __init__(self):
        self.ops = {e: [] for e in ENGINES}
        self.last_w = {}
        self.readers = {}
        self.dma_cnt = {}
        self.final_waits = []

    def _deps(self, eng, reads, writes):
        deps = set()
        for t in reads:
            p = self.last_w.get(t)
            if p is not None:
                deps.add(p)
        for t in writes:
            p = self.last_w.get(t)
            if p is not None:
                deps.add(p)
            for r in self.readers.get(t, {}).values():
                deps.add(r)
        return {d for d in deps if not (d[0] == "eng" and d[1] == eng and eng == "pe")}

    def _commit(self, me, reads, writes):
        for t in reads:
            self.readers.setdefault(t, {})[(me[0], me[1])] = me
        for t in writes:
            self.last_w[t] = me
            self.readers[t] = {}

    @staticmethod
    def _excl(reads, writes):
        reads, writes = list(reads), list(writes)
        for t in reads:
            if (t in ("plnA", "plnB") or (isinstance(t, tuple) and t[0] in ("pp", "pt", "po"))) and t not in writes:
                writes.append(t)
        return reads, writes

    def op(self, eng, fn, reads=(), writes=()):
        reads, writes = self._excl(reads, writes)
        deps = self._deps(eng, reads, writes)
        idx = len(self.ops[eng])
        self.ops[eng].append(dict(fn=fn, deps=deps, kind="op"))
        self._commit(("eng", eng, idx), reads, writes)

    def dma(self, q, fn, reads=(), writes=(), sem_key=None, inc=16):
        deps = self._deps(q, reads, writes)
        self.dma_cnt[sem_key] = self.dma_cnt.get(sem_key, 0) + inc
        me = ("dma", sem_key, self.dma_cnt[sem_key])
        self.ops[q].append(dict(fn=fn, deps=deps, kind="dma", sem_key=sem_key, inc=inc, cnt=self.dma_cnt[sem_key]))
        self._commit(me, reads, writes)
        return me

    def wait_at_end(self, eng, rec):
        self.final_waits.append((eng, rec))

    def finalize(self):
        sig = {e: set() for e in ENGINES}
        for e in ENGINES:
            for o in self.ops[e]:
                for d in o["deps"]:
                    if d[0] == "eng":
                        sig[d[1]].add(d[2])
        for e, rec in self.final_waits:
            if rec[0] == "eng":
                sig[rec[1]].add(rec[2])
        self.sigval = {}
        for e in ENGINES:
            for n, idx in enumerate(sorted(sig[e])):
                self.sigval[(e, idx)] = n + 1
        self.sig = sig
        return {e: len(sig[e]) for e in ENGINES}, dict(self.dma_cnt)

    CAP = 4096

    def ep(self, n):
        return (n - 1) // self.CAP, (n - 1) % self.CAP + 1

    def sem_ids(self):
        eng = {e: (len(self.sig[e]) + self.CAP - 1) // self.CAP for e in ENGINES}
        dma = {k: (c + self.CAP - 1) // self.CAP for k, c in self.dma_cnt.items()}
        return eng, dma

    def emit_engine(self, eng, handle, eng_sems, dma_sems):
        waited = {}

        def do_wait(d):
            if d[0] == "eng":
                key, table, val = ("e", d[1]), eng_sems, self.sigval[(d[1], d[2])]
            else:
                key, table, val = ("d", d[1]), dma_sems, d[2]
            if waited.get(key, 0) >= val:
                return
            waited[key] = val
            epoch, v = self.ep(val)
            handle.wait_ge(table[(d[1], epoch)], v)

        for idx, o in enumerate(self.ops[eng]):
            for d in sorted(o["deps"], key=repr):
                do_wait(d)
            ins = o["fn"](handle)
            if o["kind"] == "dma":
                ins.then_inc(dma_sems[(o["sem_key"], self.ep(o["cnt"])[0])], o["inc"])
            elif idx in self.sig[eng]:
                ins.then_inc(eng_sems[(eng, self.ep(self.sigval[(eng, idx)])[0])], 1)
        for e, rec in self.final_waits:
            if e == eng:
                do_wait(rec)


class Cfg:
    def __init__(self, D=2048, H=8, NT=16, TW=256, KC=31, NCORE=8, NSEG=4):
        self.D, self.H, self.NT, self.TW, self.KC = D, H, NT, TW, KC
        self.NCORE, self.NSEG = NCORE, NSEG
        self.taps = set()
        self.DK, self.DV = 256, 512
        assert D // H == self.DK and 2 * D // H == self.DV and H % 2 == 0
        self.FB = D // 128
        self.VB = 2 * D // 128
        assert self.FB % 4 == 0 and self.FB <= 16
        self.NCH = TW // 128
        self.SEG = NT * TW
        self.LC = 128 + self.SEG
        self.INC = 11 * D
        self.OFF = dict(a=0, gl=D, cg=2 * D, q=3 * D, k=4 * D, v=5 * D, rg=7 * D, ma=9 * D, mb=10 * D)
        self.WG = 512
        c = 0
        self.tc = {}
        for name, n in [("ng", 2 * self.FB), ("cw", 2 * self.FB * KC), ("cb", 2 * self.FB), ("lw", 2 * self.FB),
                        ("lb", 2 * self.FB), ("gw", 2 * self.VB), ("gb", 2 * self.VB), ("mask", H * 128),
                        ("zeta", H), ("zeta0", H), ("gch", H), ("gch0", H), ("eps", H), ("coef", NCORE * H),
                        ("rowm", 1), ("sel", NCORE), ("epsc", 1), ("ident", 128), ("ones", 128)]:
            self.tc[name] = (c, n)
            c += n
        self.NTAB = c

    def tiles(self):
        return [(0, 128)] + [(128 + i * self.TW, self.TW) for i in range(self.NT)]


def _gammas(H):
    return 1.0 - np.exp2(-5.0 - np.arange(H, dtype=np.float64))


def host_table(cfg, r, p):
    D, H, FB, VB, KC = cfg.D, cfg.H, cfg.FB, cfg.VB, cfg.KC
    t = np.zeros((128, cfg.NTAB), np.float32)

    def put(name, arr):
        c, n = cfg.tc[name]
        t[:, c:c + n] = np.asarray(arr, np.float32).reshape(128, n)

    def pfm(v):
        v = np.asarray(v)
        return np.moveaxis(v.reshape(v.shape[:-1] + (-1, 128)), -1, 0)

    put("ng", pfm(p["norm_g"]))
    cw = np.asarray(p["conv_w"])
    put("cw", np.transpose(cw.reshape(2, KC, FB, 128), (3, 0, 2, 1)))
    put("cb", pfm(p["conv_b"]))
    put("lw", pfm(p["conv_ln_w"]))
    put("lb", pfm(p["conv_ln_b"]))
    put("gw", pfm(p["ret_gn_w"]))
    put("gb", pfm(p["ret_gn_b"]))
    lg = np.log(_gammas(H))
    j = np.arange(128, dtype=np.float64)
    m = (j[:, None] <= j[None, :])[:, None, :] * np.exp(-lg[None, :, None] * (j[:, None, None] + 1.0))
    put("mask", m)
    zeta = np.exp(lg[None, :] * (127.0 - j[:, None]))
    gch = np.broadcast_to(np.exp(lg * 128.0)[None, :], (128, H))
    first = (r % cfg.NSEG == 0)
    put("zeta", zeta)
    put("gch", gch)
    put("zeta0", zeta if first else np.zeros_like(zeta))
    put("gch0", gch if first else np.ones_like(gch))
    put("eps", EPS * cfg.DK / np.exp(2.0 * lg[None, :] * (j[:, None] + 1.0)))
    coef = np.zeros((cfg.NCORE, H))
    for r2 in range(cfg.NCORE):
        if r2 // cfg.NSEG == r // cfg.NSEG and r2 < r:
            coef[r2] = np.exp(lg * cfg.SEG * (r - r2 - 1))
    put("coef", np.broadcast_to(coef.reshape(1, -1), (128, cfg.NCORE * H)))
    rowm = np.zeros((128, 1))
    if first:
        rowm[112:] = 1.0
    put("rowm", rowm)
    sel = np.zeros((128, cfg.NCORE))
    if not first:
        sel[:, r - 1] = 1.0
    put("sel", sel)
    put("epsc", np.full((128, 1), EPS))
    put("ident", np.eye(128))
    put("ones", np.full((128, 128), 1.0 / D))
    return t


def host_rope(cfg, r):
    s = r % cfg.NSEG
    pos = np.zeros(cfg.LC, np.float32)
    if s == 0:
        pos[112:128] = np.arange(16, dtype=np.float32)
    else:
        pos[:128] = 16 + s * cfg.SEG - 128 + np.arange(128, dtype=np.float32)
    pos[128:] = 16 + s * cfg.SEG + np.arange(cfg.SEG, dtype=np.float32)
    inv_freq = (1.0 / (np.float32(10000.0) ** np.linspace(0.0, 1.0, cfg.DK // 2, dtype=np.float32))).astype(np.float32)
    ang = (pos[None, :] * inv_freq[:, None]).astype(np.float32)
    return np.cos(ang).astype(np.float32), np.sin(ang).astype(np.float32)


def make_in_maps(cfg, p):
    D, H, NC, NSEG, SEG = cfg.D, cfg.H, cfg.NCORE, cfg.NSEG, cfg.SEG
    x = np.asarray(p["x"], np.float32)
    meta = np.asarray(p["meta_tokens"], np.float32)
    w_in = np.asarray(p["w_in"], np.float32)
    perm = np.arange(cfg.INC)
    for base in (cfg.OFF["q"], cfg.OFF["k"]):
        for h in range(H):
            o = base + h * cfg.DK
            perm[o:o + cfg.DK] = np.concatenate([o + np.arange(0, cfg.DK, 2), o + np.arange(1, cfg.DK, 2)])
    w_in = w_in[:, :, perm]
    RS = D // NC
    gfin = np.ascontiguousarray(np.broadcast_to(np.asarray(p["final_norm_g"], np.float32)[None, :], (128, D)))
    maps = []
    for r in range(NC):
        b, s = r // NSEG, r % NSEG
        xr = np.zeros((cfg.LC, D), np.float32)
        if s == 0:
            xr[112:128] = meta
        else:
            xr[:128] = x[b, s * SEG - 128:s * SEG]
        xr[128:] = x[b, s * SEG:(s + 1) * SEG]
        cosT, sinT = host_rope(cfg, r)
        maps.append({
            "x": xr, "tab": host_table(cfg, r, p), "cosT": cosT, "sinT": sinT, "gfin": gfin,
            "w_in_s": np.ascontiguousarray(w_in[:, r * RS:(r + 1) * RS, :]),
            "conv_out_s": np.ascontiguousarray(np.asarray(p["conv_out"], np.float32)[:, r * RS:(r + 1) * RS, :]),
            "ret_out_s": np.ascontiguousarray(np.asarray(p["ret_out"], np.float32)[:, 2 * r * RS:2 * (r + 1) * RS, :]),
            "w_out_s": np.ascontiguousarray(np.asarray(p["w_out"], np.float32)[:, r * RS:(r + 1) * RS, :]),
        })
    return maps


def build_program(cfg):
    D, H, FB, VB, KC, NCH, TW, LC, NC = cfg.D, cfg.H, cfg.FB, cfg.VB, cfg.KC, cfg.NCH, cfg.TW, cfg.LC, cfg.NCORE
    WG = cfg.WG
    CCW = min(2816, H * 512, (cfg.VB * TW) // 2)
    nc = bass.Bass("TRN2", target_bir_lowering=False)
    em = Emit()

    def din(name, shape, dt=F32):
        return nc.dram_tensor(name, shape, dt, kind="ExternalInput").ap()

    def dint(name, shape, dt):
        return nc.dram_tensor(name, shape, dt, kind="Internal").ap()

    x_d = din("x", [LC, D])
    tab_d = din("tab", [128, cfg.NTAB])
    cos_d = din("cosT", [128, LC])
    sin_d = din("sinT", [128, LC])
    gfin_d = din("gfin", [128, D])
    wdims = {"in": (D, cfg.INC), "co": (D, D), "ro": (2 * D, D), "wo": (D, D)}
    wsh = {"in": din("w_in_s", [2, D // NC, cfg.INC]), "co": din("conv_out_s", [2, D // NC, D]),
           "ro": din("ret_out_s", [2, 2 * D // NC, D]), "wo": din("w_out_s", [2, D // NC, D])}
    y_d = nc.dram_tensor("y", [cfg.SEG, D], F32, kind="ExternalOutput").ap()
    wsb = {k: [dint(f"wsb_{k}{l}", [wdims[k][0] // NC, wdims[k][1]], BF16) for l in range(2)] for k in wsh}
    W = {k: [dint(f"W_{k}{l}", list(wdims[k]), BF16) for l in range(2)] for k in wsh}
    kTs = [dint(f"kTs{l}", [128, 2 * H, LC], BF16) for l in range(2)]
    Vs = [dint(f"Vs{l}", [LC, 2 * D], BF16) for l in range(2)]
    h1_d = dint("h1", [LC, D], F32)
    sl_in = [dint(f"sl_in{l}", [128, H * 2 * 512], F32) for l in range(2)]
    sl_out = [dint(f"sl_out{l}", [NC * 128, H * 2 * 512], F32) for l in range(2)]
    hl_in = dint("hl_in", [128, D], F32)
    hl_out = dint("hl_out", [NC * 128, D], F32)

    with ExitStack() as es:
        def T(name, shape, dt):
            return es.enter_context(nc.sbuf_tensor("s_" + name, shape, dt))

        def P(name, shape, dt):
            return es.enter_context(nc.psum_tensor("p_" + name, shape, dt))

        tab = T("tab", [128, cfg.NTAB], F32)
        identb = T("identb", [128, 128], BF16)
        onesb = T("onesb", [128, 128], BF16)
        xin = T("xin", [128, NCH, D], F32)
        xs = T("xs", [128, D], BF16)
        ssq = T("ssq", [128, 4], F32)
        hnT = T("hnT", [128, FB, TW], BF16)
        NSLOT = 2
        wsl = [T(f"wsl{i}", [128, 16, WG], BF16) for i in range(NSLOT)]
        cosb = [T(f"cosb{i}", [128, TW], F32) for i in range(2)]
        sinb = [T(f"sinb{i}", [128, TW], F32) for i in range(2)]
        rt = [T(f"rt{i}", [128, TW], F32) for i in range(4)]
        kTh = [T(f"kTh{i}", [128, 2, TW], BF16) for i in range(2)]
        qTh = [T(f"qTh{i}", [128, 2, TW], BF16) for i in range(2)]
        kzh = [T(f"kzh{i}", [128, NCH, 256], BF16) for i in range(2)]
        Vh = [T(f"Vh{i}", [128, NCH, 512], BF16) for i in range(2)]
        state = T("state", [128, 2 * H * 512], F32)
        stb = [T(f"stb{i}", [128, 2 * 512], BF16) for i in range(2)]
        stg = T("stg", [128, NC, 128], F32)
        uh = T("uh", [128, FB, 32], BF16)
        ub = [T(f"ub{i}", [128, 4, 32 + TW], BF16) for i in range(2)]
        sg = [T(f"sg{i}", [128, TW], F32) for i in range(2)]
        cacc = [T(f"cacc{i}", [128, TW], F32) for i in range(2)]
        conv = T("conv", [128, FB, TW], BF16)
        sqb = [T(f"sqb{i}", [128, TW], BF16) for i in range(2)]
        lnt = [T(f"lnt{i}", [128, TW], F32) for i in range(4)]
        xn = [T(f"xn{i}", [128, TW], F32) for i in range(2)]
        gt = [T(f"gt{i}", [128, TW], BF16) for i in range(2)]
        st_ = [T(f"st{i}", [128, TW], BF16) for i in range(2)]
        yaT = T("yaT", [128, FB, TW], BF16)
        onT = T("onT", [128, VB * TW], BF16)
        rg = [T(f"rg{i}", [128, 4, TW], BF16) for i in range(2)]
        PT = [T(f"PT{i}", [128, 128], BF16) for i in range(2)]
        ontm = [T(f"ontm{i}", [128, 512], BF16) for i in range(2)]
        gnt = [T(f"gnt{i}", [128, 128], BF16) for i in range(2)]
        bst = [T(f"bst{i}", [128, 6], F32) for i in range(2)]
        bmv = [T(f"bmv{i}", [128, 4], F32) for i in range(2)]
        mt = [T(f"mt{i}", [128, TW], BF16) for i in range(4)]
        gfin = T("gfin", [128, D], F32)
        assert 2 * CCW <= 2 * H * 512 and 2 * CCW <= VB * TW
        cf = [state[:, i * CCW:(i + 1) * CCW] for i in range(2)]
        cbf = [onT[:, i * CCW:(i + 1) * CCW] for i in range(2)]

        pp = [P(f"pp{i}", [128, 512], F32) for i in range(2)]
        ptb = P("ptb", [128, 1024], BF16)
        plnA = P("plnA", [128, 512], F32)
        plnB = P("plnB", [128, 512], F32)
        po = [P(f"po{i}", [128, 512], F32) for i in range(3)]

        def tcol(name, i=0, n=1):
            c, _ = cfg.tc[name]
            return tab[:, c + i:c + i + n]

        STATE_T = [("state", h) for h in range(H)]
        ONT_T = [("onT", b) for b in range(VB)]

        def mm(out, lhsT, rhs, start, stop, r, w):
            em.op("pe", lambda e: e.matmul(out, lhsT=lhsT, rhs=rhs, start=start, stop=stop), r, w)

        def tr(out, in_, r, w):
            em.op("pe", lambda e: e.transpose(out=out, in_=in_, identity=identb[:]), r + ["identb"], w)

        def act(out, in_, func, r, w, scale=1.0, bias=0.0, accum=None):
            if accum is None:
                em.op("act", lambda e: e.activation(out=out, in_=in_, func=func, bias=bias, scale=scale), r, w)
            else:
                em.op("act", lambda e: e.activation(out=out, in_=in_, func=func, bias=bias, scale=scale,
                                                    accum_out=accum), r, w)

        def ts(eng, out, in0, s1, s2, op0, op1, r, w):
            if s2 is None:
                em.op(eng, lambda e: e.tensor_scalar(out=out, in0=in0, scalar1=s1, scalar2=None, op0=op0), r, w)
            else:
                em.op(eng, lambda e: e.tensor_scalar(out=out, in0=in0, scalar1=s1, scalar2=s2, op0=op0, op1=op1), r, w)

        def tt(eng, out, in0, in1, op, r, w):
            em.op(eng, lambda e: e.tensor_tensor(out=out, in0=in0, in1=in1, op=op), r, w)

        def stt(eng, out, in0, scalar, in1, op0, op1, r, w):
            em.op(eng, lambda e: e.scalar_tensor_tensor(out=out, in0=in0, scalar=scalar, in1=in1, op0=op0, op1=op1), r, w)

        def cp(eng, out, in_, r, w):
            em.op(eng, lambda e: e.tensor_copy(out=out, in_=in_), r, w)

        def mset(eng, ap, val, r, w):
            em.op(eng, lambda e: e.memset(ap, val), r, w)

        def dma(q, out, in_, r, w, key):
            return em.dma(q, lambda e: e.dma_start(out=out, in_=in_), r, w, sem_key=key)

        def tap(name, ap, r):
            if name not in cfg.taps:
                return
            d = nc.dram_tensor("tap_" + name, list(ap.shape), ap.dtype, kind="ExternalOutput").ap()
            rec = dma("sp", d, ap, r, [("tap", name)], ("tap", name))
            em.wait_at_end("sp", rec)

        def allgather(src, dst, r, w, key):
            em.dma("pool", lambda e: e.collective_compute("AllGather", ALU.bypass, replica_groups=[list(range(NC))],
                                                          ins=[src], outs=[dst]), r, w, sem_key=key, inc=1)

        cnt = dict(ws=0, pp=0, pt=0, po=0, rot=0, misc=0)

        def nxt(k, n):
            v = cnt[k] % n
            cnt[k] += 1
            return v

        def ppacc():
            i = nxt("pp", 2)
            return pp[i][:, 0:256], ("pp", i)

        def ptacc():
            return ptb[:, 0:512], ("pt", 0)

        def poacc():
            i = nxt("po", 3)
            return po[i], ("po", i)

        def wload(kind, l, rows0, nrows, c0, ncols):
            s = nxt("ws", NSLOT)
            for f0 in range(0, nrows // 128, 4):
                r0 = rows0 + f0 * 128
                src = W[kind][l][r0:r0 + 512, c0:c0 + ncols].rearrange("(fb p) c -> p fb c", p=128)
                dma("sp", wsl[s][:, f0:f0 + 4, 0:ncols], src, [("W", kind, l)], [("ws", s)], ("ws", s))
            return s

        dma("sp", tab[:], tab_d, [], ["tab"], "tab")
        c_id, _ = cfg.tc["ident"]
        c_on, _ = cfg.tc["ones"]
        cp("dve", identb[:], tab[:, c_id:c_id + 128], ["tab"], ["identb"])
        cp("dve", onesb[:], tab[:, c_on:c_on + 128], ["tab"], ["onesb"])
        dma("sp", gfin[:], gfin_d, [], ["gfin"], "gfin")
        ncast = 0
        cast_engs = ["dve", "act", "pool"]
        for l in range(2):
            for kind in ("in", "co", "ro", "wo"):
                R_, C_ = wdims[kind][0] // NC, wdims[kind][1]
                toks = []
                for r0 in range(0, R_, 128):
                    nr = min(128, R_ - r0)
                    for c0 in range(0, C_, CCW):
                        ncl = min(CCW, C_ - c0)
                        s = ncast % 2
                        dma("sp", cf[s][0:nr, 0:ncl], wsh[kind][l, r0:r0 + nr, c0:c0 + ncl], [], [("cf", s)], ("cf", s))
                        ce = cast_engs[ncast % 3]
                        if ce == "act":
                            act(cbf[s][0:nr, 0:ncl], cf[s][0:nr, 0:ncl], AF.Copy, [("cf", s)], [("cbf", s)])
                        else:
                            cp(ce, cbf[s][0:nr, 0:ncl], cf[s][0:nr, 0:ncl], [("cf", s)], [("cbf", s)])
                        tok = ("wsb", kind, l, ncast)
                        toks.append(tok)
                        dma("sp", wsb[kind][l][r0:r0 + nr, c0:c0 + ncl], cbf[s][0:nr, 0:ncl], [("cbf", s)], [tok], ("cbs", s))
                        ncast += 1
                allgather(wsb[kind][l], W[kind][l], toks, [("W", kind, l)], ("ccw", kind, l))
        mset("pool", state[:], 0.0, [], STATE_T + [("cf", 0), ("cf", 1)])
        mset("pool", onT[:], 0.0, [], ONT_T + [("cbf", 0), ("cbf", 1)])

        def load_x(src_rows, nch, reads=()):
            dma("sp", xin[:, 0:nch, :], src_rows.rearrange("(c p) d -> p c d", p=128), list(reads), ["xin"], "xin")

        def row_rstd(c):
            mset("dve", ssq[:, c:c + 1], 0.0, [], [("ssq", c)])
            act(xs[:], xin[:, c, :], AF.Square, ["xin", ("ssq", c)], ["xs", ("ssq", c)], accum=ssq[:, c:c + 1])
            act(ssq[:, c:c + 1], ssq[:, c:c + 1], AF.Sqrt, [("ssq", c), "tab"], [("ssq", c)], scale=1.0 / D, bias=tcol("epsc"))
            em.op("dve", lambda e, c=c: e.reciprocal(out=ssq[:, c:c + 1], in_=ssq[:, c:c + 1]), [("ssq", c)], [("ssq", c)])

        def rmsnorm_T(l, nch):
            cg, _ = cfg.tc["ng"]
            for c in range(nch):
                row_rstd(c)
                act(xs[:], xin[:, c, :], AF.Identity, ["xin", ("ssq", c)], ["xs"], scale=ssq[:, c:c + 1])
                for f0 in range(0, FB, 4):
                    pt, ptk = ptacc()
                    for k in range(4):
                        tr(pt[:, k * 128:(k + 1) * 128], xs[:, (f0 + k) * 128:(f0 + k + 1) * 128], ["xs"], [ptk])
                    for k in range(4):
                        fb = f0 + k
                        gc = cg + l * FB + fb
                        ts("dve", hnT[:, fb, c * 128:(c + 1) * 128], pt[:, k * 128:(k + 1) * 128],
                           tab[:, gc:gc + 1], None, ALU.mult, None, [ptk, "tab"], ["hnT"])

        def proj_fm(slot, cb, ntok):
            acc, k = ppacc()
            for fb in range(FB):
                mm(acc[:, 0:ntok], wsl[slot][:, fb, cb * 128:(cb + 1) * 128], hnT[:, fb, 0:ntok], fb == 0, fb == FB - 1,
                   [("ws", slot), "hnT"], [k])
            return acc[:, 0:ntok], k

        def load_rope(off, ntok):
            i = nxt("rot", 2)
            dma("sp", cosb[i][:, 0:ntok], cos_d[:, off:off + ntok], [], [("cos", i)], ("cos", i))
            dma("sp", sinb[i][:, 0:ntok], sin_d[:, off:off + ntok], [], [("sin", i)], ("sin", i))
            return i

        def rotary(p0, k0, p1, k1, ri, dst, dtok, ntok):
            c_, s_ = cosb[ri][:, 0:ntok], sinb[ri][:, 0:ntok]
            tt("dve", rt[0][:, 0:ntok], p0, c_, ALU.mult, [k0, ("cos", ri)], [("rt", 0)])
            tt("dve", rt[1][:, 0:ntok], p1, s_, ALU.mult, [k1, ("sin", ri)], [("rt", 1)])
            tt("dve", rt[2][:, 0:ntok], p1, c_, ALU.mult, [k1, ("cos", ri)], [("rt", 2)])
            tt("dve", rt[3][:, 0:ntok], p0, s_, ALU.mult, [k0, ("sin", ri)], [("rt", 3)])
            tt("pool", dst[:, 0, 0:ntok], rt[0][:, 0:ntok], rt[1][:, 0:ntok], ALU.subtract, [("rt", 0), ("rt", 1)], [dtok])
            tt("pool", dst[:, 1, 0:ntok], rt[2][:, 0:ntok], rt[3][:, 0:ntok], ALU.add, [("rt", 2), ("rt", 3)], [dtok])

        def make_kz(i, h, nch, small):
            zname = "zeta0" if small else "zeta"
            pt, ptk = ptacc()
            for c in range(nch):
                for half in range(2):
                    q = c * 2 + half
                    tr(pt[:, q * 128:(q + 1) * 128], kTh[i][:, half, c * 128:(c + 1) * 128], [("kTh", i)], [ptk])
            for c in range(nch):
                act(kzh[i][:, c, :], pt[:, c * 256:(c + 1) * 256], AF.Identity, [ptk, "tab"], [("kzh", i)], scale=tcol(zname, h))

        def state_update(i, h, c, small):
            gname = "gch0" if small else "gch"
            for half in range(2):
                a = 2 * h + half
                acc, k = poacc()
                mm(acc[:], kzh[i][:, c, half * 128:(half + 1) * 128], Vh[i][:, c, :], True, True, [("kzh", i), ("Vh", i)], [k])
                stt("dve", state[:, a * 512:(a + 1) * 512], state[:, a * 512:(a + 1) * 512], tcol(gname, h), acc[:],
                    ALU.mult, ALU.add, [k, ("state", h), "tab"], [("state", h)])

        def pass_A(l, ti, off, ntok):
            small = (ti == 0)
            nch = ntok // 128
            ri = load_rope(off, ntok)
            for g in range(H // 2):
                s = wload("in", l, 0, D, cfg.OFF["k"] + g * WG, WG)
                for hh in range(2):
                    h = 2 * g + hh
                    i = hh
                    p0, k0 = proj_fm(s, 2 * hh, ntok)
                    p1, k1 = proj_fm(s, 2 * hh + 1, ntok)
                    rotary(p0, k0, p1, k1, ri, kTh[i], ("kTh", i), ntok)
                    dma("sp", kTs[l][:, 2 * h:2 * h + 2, off:off + ntok], kTh[i][:, :, 0:ntok], [("kTh", i)],
                        [("kTs", l, ti, h)], ("kTst", i))
                    make_kz(i, h, nch, small)
                for hh in range(2):
                    h = 2 * g + hh
                    i = hh
                    s = wload("in", l, 0, D, cfg.OFF["v"] + h * 512, 512)
                    for c in range(nch):
                        acc, k = poacc()
                        for fb in range(FB):
                            mm(acc[:], hnT[:, fb, c * 128:(c + 1) * 128], wsl[s][:, fb, :], fb == 0, fb == FB - 1,
                               [("ws", s), "hnT"], [k])
                        act(Vh[i][:, c, :], acc[:], AF.Copy, [k], [("Vh", i)])
                    dma("sp", Vs[l][off:off + ntok, h * 512:(h + 1) * 512].rearrange("(c p) v -> p c v", p=128),
                        Vh[i][:, 0:nch, :], [("Vh", i)], [("Vs", l, ti, h)], ("Vst", i))
                    for c in range(nch):
                        state_update(i, h, c, small)

        def pass_B(l, ti, off, ntok):
            small = (ti == 0)
            nch = ntok // 128
            ccw, _ = cfg.tc["cw"]
            ccb, _ = cfg.tc["cb"]
            clw, _ = cfg.tc["lw"]
            clb, _ = cfg.tc["lb"]
            cgw, _ = cfg.tc["gw"]
            cgb, _ = cfg.tc["gb"]
            cm, _ = cfg.tc["mask"]
            for g in range(FB // 4):
                sgl = wload("in", l, 0, D, cfg.OFF["gl"] + g * WG, WG)
                sa = wload("in", l, 0, D, cfg.OFF["a"] + g * WG, WG)
                ui = g % 2
                cp("pool", ub[ui][:, :, 0:32], uh[:, 4 * g:4 * g + 4, :], ["uh"], [("ub", ui)])
                for j in range(4):
                    pg, kg = proj_fm(sgl, j, ntok)
                    si = nxt("misc", 2)
                    act(sg[si][:, 0:ntok], pg, AF.Sigmoid, [kg], [("sg", si)])
                    pa, ka = proj_fm(sa, j, ntok)
                    tt("dve", ub[ui][:, j, 32:32 + ntok], pa, sg[si][:, 0:ntok], ALU.mult, [ka, ("sg", si)], [("ub", ui)])
                cp("pool", uh[:, 4 * g:4 * g + 4, :], ub[ui][:, :, ntok:ntok + 32], [("ub", ui)], ["uh"])
                for j in range(4):
                    cb = 4 * g + j
                    eng = "dve"
                    ai = j % 2
                    wc = ccw + (l * FB + cb) * KC
                    bc = ccb + l * FB + cb
                    ts(eng, cacc[ai][:, 0:ntok], ub[ui][:, j, 2:2 + ntok], tab[:, wc:wc + 1], tab[:, bc:bc + 1],
                       ALU.mult, ALU.add, [("ub", ui), "tab"], [("cacc", ai)])
                    for k in range(1, KC):
                        last = (k == KC - 1)
                        out = conv[:, cb, 0:ntok] if last else cacc[ai][:, 0:ntok]
                        stt(eng, out, ub[ui][:, j, 2 + k:2 + k + ntok], tab[:, wc + k:wc + k + 1], cacc[ai][:, 0:ntok],
                            ALU.mult, ALU.add, [("ub", ui), ("cacc", ai), "tab"], [("conv", cb)] if last else [("cacc", ai)])
                    qi = j % 2
                    tt("pool", sqb[qi][:, 0:ntok], conv[:, cb, 0:ntok], conv[:, cb, 0:ntok], ALU.mult, [("conv", cb)], [("sqb", qi)])
                    mm(plnA[:, 0:ntok], onesb[:], conv[:, cb, 0:ntok], cb == 0, cb == FB - 1, ["onesb", ("conv", cb)], ["plnA"])
                    mm(plnB[:, 0:ntok], onesb[:], sqb[qi][:, 0:ntok], cb == 0, cb == FB - 1, ["onesb", ("sqb", qi)], ["plnB"])
            mean, m2, rstd, nmr = (lnt[i][:, 0:ntok] for i in range(4))
            act(mean, plnA[:, 0:ntok], AF.Copy, ["plnA"], [("lnt", 0)])
            tt("pool", m2, mean, mean, ALU.mult, [("lnt", 0)], [("lnt", 1)])
            tt("dve", rstd, plnB[:, 0:ntok], m2, ALU.subtract, ["plnB", ("lnt", 1)], [("lnt", 2)])
            act(rstd, rstd, AF.Sqrt, [("lnt", 2), "tab"], [("lnt", 2)], bias=tcol("epsc"))
            em.op("dve", lambda e, rstd=rstd: e.reciprocal(out=rstd, in_=rstd), [("lnt", 2)], [("lnt", 2)])
            stt("dve", nmr, mean, -1.0, rstd, ALU.mult, ALU.mult, [("lnt", 0), ("lnt", 2)], [("lnt", 3)])
            for g in range(FB // 4):
                scg = wload("in", l, 0, D, cfg.OFF["cg"] + g * WG, WG)
                for j in range(4):
                    cb = 4 * g + j
                    pgt, kgt = proj_fm(scg, j, ntok)
                    i2 = j % 2
                    act(gt[i2][:, 0:ntok], pgt, AF.Silu, [kgt], [("gt", i2)])
                    tt("dve", xn[i2][:, 0:ntok], conv[:, cb, 0:ntok], rstd, ALU.mult, [("conv", cb), ("lnt", 2)], [("xn", i2)])
                    tt("pool", xn[i2][:, 0:ntok], xn[i2][:, 0:ntok], nmr, ALU.add, [("xn", i2), ("lnt", 3)], [("xn", i2)])
                    lwc, lbc = clw + l * FB + cb, clb + l * FB + cb
                    act(st_[i2][:, 0:ntok], xn[i2][:, 0:ntok], AF.Silu, [("xn", i2), "tab"], [("st", i2)],
                        scale=tab[:, lwc:lwc + 1], bias=tab[:, lbc:lbc + 1])
                    tt("pool", conv[:, cb, 0:ntok], st_[i2][:, 0:ntok], gt[i2][:, 0:ntok], ALU.mult, [("st", i2), ("gt", i2)], [("conv", cb)])
            for g in range(FB // 4):
                s = wload("co", l, 0, D, g * WG, WG)
                for j in range(4):
                    acc, k = ppacc()
                    for fb in range(FB):
                        mm(acc[:, 0:ntok], wsl[s][:, fb, j * 128:(j + 1) * 128], conv[:, fb, 0:ntok], fb == 0, fb == FB - 1,
                           [("ws", s), ("conv", fb)], [k])
                    act(yaT[:, 4 * g + j, 0:ntok], acc[:, 0:ntok], AF.Copy, [k], [("yaT", 4 * g + j)])
            ri = load_rope(off, ntok)
            sq_ = None
            for h in range(H):
                i = h % 2
                dma("sp", kTh[i][:, :, 0:ntok], kTs[l][:, 2 * h:2 * h + 2, off:off + ntok], [("kTs", l, ti, h)], [("kTh", i)], ("kTld", i))
                dma("sp", Vh[i][:, 0:nch, :], Vs[l][off:off + ntok, h * 512:(h + 1) * 512].rearrange("(c p) v -> p c v", p=128),
                    [("Vs", l, ti, h)], [("Vh", i)], ("Vld", i))
                if i == 0:
                    sq_ = wload("in", l, 0, D, cfg.OFF["q"] + (h // 2) * WG, WG)
                p0, k0 = proj_fm(sq_, 2 * i, ntok)
                p1, k1 = proj_fm(sq_, 2 * i + 1, ntok)
                rotary(p0, k0, p1, k1, ri, qTh[i], ("qTh", i), ntok)
                srg = wload("in", l, 0, D, cfg.OFF["rg"] + h * 512, 512)
                for j in range(4):
                    pr, kr = proj_fm(srg, j, ntok)
                    act(rg[i][:, j, 0:ntok], pr, AF.Silu, [kr], [("rg", i)])
                make_kz(i, h, nch, small)
                tap(f"pB{l}_q_t{ti}_h{h}", qTh[i][:, :, 0:ntok], [("qTh", i)])
                tap(f"pB{l}_k_t{ti}_h{h}", kTh[i][:, :, 0:ntok], [("kTh", i)])
                tap(f"pB{l}_rg_t{ti}_h{h}", rg[i][:, :, 0:ntok], [("rg", i)])
                for c in range(nch):
                    cs = slice(c * 128, (c + 1) * 128)
                    bi = nxt("misc", 2)
                    cp("pool", stb[bi][:], state[:, 2 * h * 512:(2 * h + 2) * 512], [("state", h)], [("stb", bi)])
                    tap(f"pB{l}_stb_t{ti}_h{h}_c{c}", stb[bi][:], [("stb", bi)])
                    sT, kS = ppacc()
                    for half in range(2):
                        mm(sT[:, 0:128], kTh[i][:, half, cs], qTh[i][:, half, cs], half == 0, half == 1, [("kTh", i), ("qTh", i)], [kS])
                    tt("dve", PT[bi][:], sT[:, 0:128], tab[:, cm + h * 128:cm + (h + 1) * 128], ALU.mult, [kS, "tab"], [("PT", bi)])
                    tap(f"pB{l}_PT_t{ti}_h{h}_c{c}", PT[bi][:], [("PT", bi)])
                    oacc, ko = poacc()
                    mm(oacc[:], PT[bi][:], Vh[i][:, c, :], True, False, [("PT", bi), ("Vh", i)], [ko])
                    for half in range(2):
                        mm(oacc[:], qTh[i][:, half, cs], stb[bi][:, half * 512:(half + 1) * 512], False, half == 1,
                           [("qTh", i), ("stb", bi)], [ko])
                    em.op("dve", lambda e, bi=bi, oacc=oacc: e.bn_stats(out=bst[bi][:], in_=oacc[:]), [ko], [("bst", bi)])
                    em.op("dve", lambda e, bi=bi: e.bn_aggr(out=bmv[bi][:, 0:2], in_=bst[bi][:]), [("bst", bi)], [("bmv", bi)])
                    act(bmv[bi][:, 2:3], bmv[bi][:, 1:2], AF.Sqrt, [("bmv", bi), "tab"], [("bmv", bi)], bias=tcol("eps", h))
                    em.op("dve", lambda e, bi=bi: e.reciprocal(out=bmv[bi][:, 2:3], in_=bmv[bi][:, 2:3]), [("bmv", bi)], [("bmv", bi)])
                    ts("dve", ontm[bi][:], oacc[:], bmv[bi][:, 0:1], bmv[bi][:, 2:3], ALU.subtract, ALU.mult, [ko, ("bmv", bi)], [("ontm", bi)])
                    tap(f"pB{l}_ontm_t{ti}_h{h}_c{c}", ontm[bi][:], [("ontm", bi)])
                    pt, ptk = ptacc()
                    for j in range(4):
                        tr(pt[:, j * 128:(j + 1) * 128], ontm[bi][:, j * 128:(j + 1) * 128], [("ontm", bi)], [ptk])
                    for j in range(4):
                        blk = 4 * h + j
                        gi = j % 2
                        gwc, gbc = cgw + l * VB + blk, cgb + l * VB + blk
                        act(gnt[gi][:], pt[:, j * 128:(j + 1) * 128], AF.Identity, [ptk, "tab"], [("gnt", gi)],
                            scale=tab[:, gwc:gwc + 1], bias=tab[:, gbc:gbc + 1])
                        tt("pool", onT[:, blk * TW + c * 128:blk * TW + (c + 1) * 128], gnt[gi][:], rg[i][:, j, cs], ALU.mult,
                           [("gnt", gi), ("rg", i)], [("onT", blk)])
                    state_update(i, h, c, small)
            for g in range(FB // 4):
                s0 = wload("ro", l, 0, D, g * WG, WG)
                s1 = wload("ro", l, D, D, g * WG, WG)
                for j in range(4):
                    acc, k = ppacc()
                    for fb in range(VB):
                        s = s0 if fb < FB else s1
                        mm(acc[:, 0:ntok], wsl[s][:, fb % FB, j * 128:(j + 1) * 128], onT[:, fb * TW:fb * TW + ntok],
                           fb == 0, fb == VB - 1, [("ws", s), ("onT", fb)], [k])
                    act(conv[:, 4 * g + j, 0:ntok], acc[:, 0:ntok], AF.Copy, [k], [("conv", 4 * g + j)])
            for g in range(FB // 4):
                sma = wload("in", l, 0, D, cfg.OFF["ma"] + g * WG, WG)
                smb = wload("in", l, 0, D, cfg.OFF["mb"] + g * WG, WG)
                for j in range(4):
                    db = 4 * g + j
                    pa, ka = proj_fm(sma, j, ntok)
                    act(mt[0][:, 0:ntok], pa, AF.Sigmoid, [ka], [("mt", 0)])
                    pb, kb = proj_fm(smb, j, ntok)
                    act(mt[1][:, 0:ntok], pb, AF.Sigmoid, [kb], [("mt", 1)])
                    tt("pool", mt[2][:, 0:ntok], mt[0][:, 0:ntok], yaT[:, db, 0:ntok], ALU.mult, [("mt", 0), ("yaT", db)], [("mt", 2)])
                    tt("pool", mt[3][:, 0:ntok], mt[1][:, 0:ntok], conv[:, db, 0:ntok], ALU.mult, [("mt", 1), ("conv", db)], [("mt", 3)])
                    tt("pool", yaT[:, db, 0:ntok], mt[2][:, 0:ntok], mt[3][:, 0:ntok], ALU.add, [("mt", 2), ("mt", 3)], [("yaT", db)])
            yat = [("yaT", db) for db in range(FB)]
            for g in range(FB // 4):
                s = wload("wo", l, 0, D, g * WG, WG)
                for c in range(nch):
                    acc, k = poacc()
                    for fb in range(FB):
                        mm(acc[:], yaT[:, fb, c * 128:(c + 1) * 128], wsl[s][:, fb, :], fb == 0, fb == FB - 1, [("ws", s), ("yaT", fb)], [k])
                    tt("dve", xin[:, c, g * WG:(g + 1) * WG], xin[:, c, g * WG:(g + 1) * WG], acc[:], ALU.add, [k, "xin"], ["xin"])

        def publish_state(l):
            dma("sp", sl_in[l], state[:], STATE_T, [("sl_in", l)], ("slst", l))
            allgather(sl_in[l], sl_out[l], [("sl_in", l)], [("sl_out", l)], ("ccs", l))

        def combine_state(l):
            cc, _ = cfg.tc["coef"]
            src = sl_out[l].rearrange("(r p) f -> p r f", p=128)
            for a in range(2 * H):
                h = a // 2
                for q in range(4):
                    lo = a * 512 + q * 128
                    dma("sp", stg[:], src[:, :, lo:lo + 128], [("sl_out", l)], ["stg"], "stg")
                    dst = state[:, lo:lo + 128]
                    ts("dve", dst, stg[:, 0, :], tab[:, cc + h:cc + h + 1], None, ALU.mult, None, ["stg", "tab"], [("state", h)])
                    for r in range(1, NC):
                        cr = cc + r * H + h
                        stt("dve", dst, stg[:, r, :], tab[:, cr:cr + 1], dst, ALU.mult, ALU.add, ["stg", "tab"], [("state", h)])

        tiles = cfg.tiles()
        last_ti = len(tiles) - 1
        for ti, (off, ntok) in enumerate(tiles):
            load_x(x_d[off:off + ntok, :], ntok // 128)
            rmsnorm_T(0, ntok // 128)
            tap(f"p0_rstd_t{ti}", ssq[:, 0:1], [("ssq", 0)])
            tap(f"p0_hnT_t{ti}", hnT[:, :, 0:ntok], ["hnT"])
            pass_A(0, ti, off, ntok)
            tap(f"p0_kT_t{ti}", kTh[1][:, :, 0:ntok], [("kTh", 1)])
            tap(f"p0_V_t{ti}", Vh[1][:, 0:ntok // 128, :], [("Vh", 1)])
            tap(f"p0_state_t{ti}", state[:], STATE_T)
        publish_state(0)
        combine_state(0)
        tap("p0_sin", state[:], STATE_T)
        mset("pool", uh[:], 0.0, [], ["uh"])
        for ti, (off, ntok) in enumerate(tiles):
            nch = ntok // 128
            load_x(x_d[off:off + ntok, :], nch)
            rmsnorm_T(0, nch)
            pass_B(0, ti, off, ntok)
            tap(f"p1_conv_t{ti}", conv[:, :, 0:ntok], [("conv", cb) for cb in range(FB)])
            tap(f"p1_onT_t{ti}", onT[:], ONT_T)
            tap(f"p1_mT_t{ti}", yaT[:, :, 0:ntok], [("yaT", db) for db in range(FB)])
            tap(f"p1_h1_t{ti}", xin[:, 0:nch, :], ["xin"])
            if ti == 0:
                ts("dve", xin[:, 0, :], xin[:, 0, :], tcol("rowm"), None, ALU.mult, None, ["xin", "tab"], ["xin"])
            dma("sp", h1_d[off:off + ntok, :].rearrange("(c p) d -> p c d", p=128), xin[:, 0:nch, :], ["xin"], [("h1", ti)], "h1st")
            if ti == last_ti:
                dma("sp", hl_in, xin[:, nch - 1, :], ["xin"], ["hl_in"], "hlst")
        allgather(hl_in, hl_out, ["hl_in"], ["hl_out"], "cch")
        mset("pool", state[:], 0.0, [], STATE_T)
        for ti, (off, ntok) in enumerate(tiles):
            load_x(h1_d[off:off + ntok, :], ntok // 128, [("h1", ti)])
            rmsnorm_T(1, ntok // 128)
            pass_A(1, ti, off, ntok)
        publish_state(1)
        combine_state(1)
        mset("pool", uh[:], 0.0, [], ["uh"])
        ylast = None
        for ti, (off, ntok) in enumerate(tiles):
            nch = ntok // 128
            load_x(h1_d[off:off + ntok, :], nch, [("h1", ti)])
            if ti == 0:
                cs_, _ = cfg.tc["sel"]
                ts("dve", xin[:, 0, :], xin[:, 0, :], tcol("rowm"), None, ALU.mult, None, ["xin", "tab"], ["xin"])
                src = hl_out.rearrange("(r p) d -> p r d", p=128)
                for r in range(NC):
                    dma("sp", xin[:, 1, :], src[:, r, :], ["hl_out"], ["xin1"], "xin1")
                    stt("dve", xin[:, 0, :], xin[:, 1, :], tab[:, cs_ + r:cs_ + r + 1], xin[:, 0, :], ALU.mult, ALU.add,
                        ["xin1", "xin", "tab"], ["xin"])
            rmsnorm_T(1, nch)
            pass_B(1, ti, off, ntok)
            if ti > 0:
                for c in range(nch):
                    row_rstd(c)
                    stt("dve", xin[:, c, :], xin[:, c, :], ssq[:, c:c + 1], gfin[:], ALU.mult, ALU.mult,
                        ["xin", ("ssq", c), "gfin"], ["xin"])
                ylast = dma("sp", y_d[off - 128:off - 128 + ntok, :].rearrange("(c p) d -> p c d", p=128), xin[:, 0:nch, :],
                            ["xin"], [("y", ti)], "yst")
        em.wait_at_end("sp", ylast)

        nsig, dcnt = em.finalize()
        n_eng, n_dma = em.sem_ids()
        eng_sems = {(e, j): es.enter_context(nc.semaphore(f"se_{e}_{j}")) for e in ENGINES for j in range(n_eng[e])}
        dma_sems = {}
        for n, k in enumerate(dcnt):
            for j in range(n_dma[k]):
                dma_sems[(k, j)] = es.enter_context(nc.semaphore(f"sd_{n}_{j}"))
        block = es.enter_context(nc.Block())

        @block.sync
        def _(h):
            em.emit_engine("sp", h, eng_sems, dma_sems)

        @block.tensor
        def _(h):
            em.emit_engine("pe", h, eng_sems, dma_sems)

        @block.vector
        def _(h):
            em.emit_engine("dve", h, eng_sems, dma_sems)

        @block.scalar
        def _(h):
            em.emit_engine("act", h, eng_sems, dma_sems)

        @block.gpsimd
        def _(h):
            em.emit_engine("pool", h, eng_sems, dma_sems)

    stats = dict(ops={e: len(em.ops[e]) for e in ENGINES}, signals=nsig,
                 n_sems=len(eng_sems) + len(dma_sems), max_sem=min(Emit.CAP, max([0] + list(nsig.values()) + list(dcnt.values()))))
    return nc, stats


def run_cfg(cfg, p):
    nc, stats = build_program(cfg)
    res = run_bass_kernel_spmd(nc, make_in_maps(cfg, p), core_ids=list(range(cfg.NCORE)))
    nb = cfg.NCORE // cfg.NSEG
    out = np.zeros((nb, cfg.NSEG * cfg.SEG, cfg.D), np.float32)
    for r in range(cfg.NCORE):
        b, s = r // cfg.NSEG, r % cfg.NSEG
        out[b, s * cfg.SEG:(s + 1) * cfg.SEG] = res.results[r]["y"]
    return out


def kernel(**inputs):
    return run_cfg(Cfg(), inputs)
```
